# Optimizing a Trainium2 kernel written in Bass

```python
import math
import jax, jax.numpy as jnp
from jax import lax
import numpy as np

D_MODEL = 2048
BATCH = 4
SEQ = 2048
DEPTH = 4

GRID_W = 64
CTX_LEN = 256
N_MIXERS = 3
N_MOD = 6
MLP_HIDDEN = 4 * D_MODEL
NORM_EPS = 1e-6
FNET_GROUPS = 8
FNET_GROUP_DIM = D_MODEL // FNET_GROUPS
HGRN_EXPAND = 128
HGRN_HEADS = D_MODEL // HGRN_EXPAND
HGRN_DK = HGRN_EXPAND
HGRN_DV = D_MODEL // HGRN_HEADS
HGRN_F = HGRN_HEADS * HGRN_DK
HGRN_CHUNK = 16
DIFF_HEAD_DIM = 128
DIFF_HEADS = D_MODEL // (2 * DIFF_HEAD_DIM)
DIFF_V_DIM = 2 * DIFF_HEAD_DIM
Q_BLOCK = 128
ROPE_BASE = 10000.0

kernel_name = 'hybrid_fnet_hgrn2_diffattn_dit'

F32 = jnp.float32


def rms_norm(x, g):
    xf = x.astype(F32)
    y = xf * lax.rsqrt(jnp.mean(xf * xf, axis=-1, keepdims=True) + NORM_EPS)
    return (y * g.astype(F32)).astype(x.dtype)


def modulate(h, shift, scale):
    return h * (1 + scale) + shift


def sqrelu_mlp(h, w1, w2):
    return jnp.square(jax.nn.relu(h @ w1)) @ w2


def axial_rope_tables(n_tokens, dim):
    rows = n_tokens // GRID_W
    row = jnp.broadcast_to(jnp.arange(rows)[:, None], (rows, GRID_W)).reshape(-1).astype(F32)
    col = jnp.broadcast_to(jnp.arange(GRID_W)[None, :], (rows, GRID_W)).reshape(-1).astype(F32)
    n_freq = dim // 4
    inv = ROPE_BASE ** (-jnp.arange(n_freq, dtype=F32) / n_freq)
    ang = jnp.concatenate([row[:, None] * inv, col[:, None] * inv], axis=-1)
    return jnp.cos(ang), jnp.sin(ang)


def apply_rope(x, cos, sin):
    xf = x.astype(F32)
    x1, x2 = jnp.split(xf, 2, axis=-1)
    return jnp.concatenate([x1 * cos - x2 * sin, x1 * sin + x2 * cos], axis=-1).astype(x.dtype)


def fourier_mix(h):
    B, L, D = h.shape
    hg = h.astype(F32).reshape(B, L, FNET_GROUPS, FNET_GROUP_DIM)
    return jnp.fft.fftn(hg, axes=(1, 3)).real.reshape(B, L, D).astype(h.dtype)


def hgrn_lower_bounds(p):
    cs = jnp.cumsum(jax.nn.softmax(p.astype(F32), axis=0), axis=0)
    return cs - cs[0:1]


def chunk_gla(q, k, v, logf, s0):
    B, H, L, dk = q.shape
    C = HGRN_CHUNK
    N = L // C
    r = lambda t: jnp.moveaxis(t.astype(F32).reshape(B, H, N, C, t.shape[-1]), 2, 0)
    bcum = jnp.cumsum(logf.astype(F32).reshape(B, H, N, C, dk), axis=3)
    xs = (r(q), r(k), r(v), jnp.moveaxis(bcum, 2, 0))
    mask = jnp.tril(jnp.ones((C, C), dtype=bool))[:, :, None]

    def step(s, inp):
        qc, kc, vc, bc = inp
        bl = bc[:, :, -1:, :]
        decay = jnp.exp(jnp.where(mask, bc[:, :, :, None, :] - bc[:, :, None, :, :], -jnp.inf))
        a = jnp.einsum('bhtd,bhsd,bhtsd->bhts', qc, kc, decay)
        o = jnp.einsum('bhts,bhsv->bhtv', a, vc) + jnp.einsum('bhtd,bhdv->bhtv', qc * jnp.exp(bc), s)
        s = jnp.exp(bl[:, :, 0, :])[..., None] * s + jnp.einsum('bhsd,bhsv->bhdv', kc * jnp.exp(bl - bc), vc)
        return s, o

    s_final, o = lax.scan(step, s0.astype(F32), xs)
    return jnp.moveaxis(o, 0, 2).reshape(B, H, L, v.shape[-1]), s_final


def hgrn2_mixer(h_lat, h_ctx, w_in, lb, gnorm, w_out, ctx_out):
    lb = lb.reshape(2, HGRN_HEADS, HGRN_DK)

    def project(h):
        B, L, _ = h.shape
        q, zf, zb, v, g = jnp.split(h @ w_in, [HGRN_F, 2 * HGRN_F, 3 * HGRN_F, 3 * HGRN_F + D_MODEL], axis=-1)
        hd = lambda t: t.reshape(B, L, HGRN_HEADS, -1).transpose(0, 2, 1, 3)
        return hd(q), hd(zf), hd(zb), hd(v), hd(g)

    def forget(z, lbd):
        zf = z.astype(F32)
        l = lbd[None, :, None, :]
        logf = jnp.logaddexp(jnp.log(l), jnp.log1p(-l) + jax.nn.log_sigmoid(zf))
        key = (1 - l) * jax.nn.sigmoid(-zf)
        return logf, key

    def readout(o, g, dtype):
        B, H, L, _ = o.shape
        y = rms_norm(o, gnorm) * jax.nn.silu(g.astype(F32))
        return y.transpose(0, 2, 1, 3).reshape(B, L, H * HGRN_DV).astype(dtype) @ w_out

    flip = lambda t: jnp.flip(t, axis=2)
    qc, zfc, zbc, vc, gc = project(h_ctx)
    ql, zfl, zbl, vl, gl = project(h_lat)
    s0 = jnp.zeros((qc.shape[0], HGRN_HEADS, HGRN_DK, HGRN_DV), F32)
    lfc, kfc = forget(zfc, lb[0])
    o_cf, s_f = chunk_gla(qc, kfc, vc, lfc, s0)
    lfl, kfl = forget(zfl, lb[0])
    o_lf, _ = chunk_gla(ql, kfl, vl, lfl, s_f)
    lbc, kbc = forget(zbc, lb[1])
    o_cb, s_b = chunk_gla(flip(qc), flip(kbc), flip(vc), flip(lbc), s0)
    lbl, kbl = forget(zbl, lb[1])
    o_lb, _ = chunk_gla(flip(ql), flip(kbl), flip(vl), flip(lbl), s_b)
    y_lat = readout(o_lf + flip(o_lb), gl, h_lat.dtype)
    y_ctx = readout(o_cf + flip(o_cb), gc, h_ctx.dtype) if ctx_out else None
    return y_lat, y_ctx


def diff_core(q, k, v, lam):
    s = jnp.einsum('bhmqd,bhmkd->bhmqk', q, k).astype(F32) * (DIFF_HEAD_DIM ** -0.5)
    p = jax.nn.softmax(s, axis=-1)
    a = p[:, :, 0] - lam * p[:, :, 1]
    return jnp.einsum('bhqk,bhkv->bhqv', a.astype(v.dtype), v)


def diff_attention(h_lat, h_ctx, w_qkv, lam_p, subln, w_out, lambda_init, cos, sin, ctx_out):
    def project(h):
        B, L, _ = h.shape
        q, k, v = jnp.split(h @ w_qkv, 3, axis=-1)
        q = q.reshape(B, L, DIFF_HEADS, 2, DIFF_HEAD_DIM).transpose(0, 2, 3, 1, 4)
        k = k.reshape(B, L, DIFF_HEADS, 2, DIFF_HEAD_DIM).transpose(0, 2, 3, 1, 4)
        v = v.reshape(B, L, DIFF_HEADS, DIFF_V_DIM).transpose(0, 2, 1, 3)
        return q, k, v

    def merge(o, dtype):
        B, H, L, _ = o.shape
        o = rms_norm(o, subln) * (1 - lambda_init)
        return o.transpose(0, 2, 1, 3).reshape(B, L, H * DIFF_V_DIM).astype(dtype) @ w_out

    lp = lam_p.astype(F32)
    lam = jnp.exp(jnp.sum(lp[0] * lp[1])) - jnp.exp(jnp.sum(lp[2] * lp[3])) + lambda_init
    q_c, k_c, v_c = project(h_ctx)
    q_l, k_l, v_l = project(h_lat)
    q_l, k_l = apply_rope(q_l, cos, sin), apply_rope(k_l, cos, sin)
    k_all = jnp.concatenate([k_c, k_l], axis=3)
    v_all = jnp.concatenate([v_c, v_l], axis=2)
    B, H, _, L, d = q_l.shape
    nb = L // Q_BLOCK
    q_blocks = jnp.moveaxis(q_l.reshape(B, H, 2, nb, Q_BLOCK, d), 3, 0)
    o_blocks = lax.map(lambda qb: diff_core(qb, k_all, v_all, lam), q_blocks)
    o_l = jnp.moveaxis(o_blocks, 0, 2).reshape(B, H, L, DIFF_V_DIM)
    y_lat = merge(o_l, h_lat.dtype)
    y_ctx = merge(diff_core(q_c, k_c, v_c, lam), h_ctx.dtype) if ctx_out else None
    return y_lat, y_ctx


def setup_inputs(seed: int = 0) -> dict:
    key = jax.random.key(seed)
    ks = jax.random.split(key, 20)
    n_slot = lambda m: len(range(m, DEPTH, N_MIXERS))
    n_a, n_b, n_c = n_slot(0), n_slot(1), n_slot(2)
    D = D_MODEL
    nrm = lambda k, shape, s: jax.random.normal(k, shape, F32) * s
    return {
        'x': nrm(ks[0], (BATCH, SEQ, D), 1.0),
        'c': nrm(ks[1], (BATCH, D), 1.0),
        'ctx': nrm(ks[2], (BATCH, CTX_LEN, D), 1.0),
        'c_ctx': nrm(ks[3], (D,), 1.0),
        'w_mod': nrm(ks[4], (DEPTH, D, N_MOD * D), D ** -0.5),
        'b_mod': nrm(ks[5], (DEPTH, N_MOD * D), 0.01),
        'norm_g': 1.0 + nrm(ks[6], (DEPTH, 4, D), 0.02),
        'w_mlp_in': nrm(ks[7], (DEPTH, D, MLP_HIDDEN), D ** -0.5),
        'w_mlp_out': nrm(ks[8], (DEPTH, MLP_HIDDEN, D), MLP_HIDDEN ** -0.5),
        'fnet_w_out': nrm(ks[9], (n_a, D, D), D ** -0.5),
        'hgrn_w_in': nrm(ks[10], (n_b, D, 3 * HGRN_F + 2 * D), D ** -0.5),
        'hgrn_lb': nrm(ks[11], (DEPTH, 2, HGRN_F), 0.1),
        'hgrn_gnorm': 1.0 + nrm(ks[12], (n_b, HGRN_DV), 0.02),
        'hgrn_w_out': nrm(ks[13], (n_b, D, D), D ** -0.5),
        'diff_w_qkv': nrm(ks[14], (n_c, D, 3 * D), D ** -0.5),
        'diff_lambda': nrm(ks[15], (n_c, 4, DIFF_HEAD_DIM), 0.1),
        'diff_subln': 1.0 + nrm(ks[16], (n_c, DIFF_V_DIM), 0.02),
        'diff_w_out': nrm(ks[17], (n_c, D, D), D ** -0.5),
    }


def reference(x, c, ctx, c_ctx, w_mod, b_mod, norm_g, w_mlp_in, w_mlp_out, fnet_w_out, hgrn_w_in, hgrn_lb, hgrn_gnorm, hgrn_w_out, diff_w_qkv, diff_lambda, diff_subln, diff_w_out):
    L = x.shape[1]
    cos, sin = axial_rope_tables(L, DIFF_HEAD_DIM)
    lower_bounds = hgrn_lower_bounds(hgrn_lb)
    silu_c = jax.nn.silu(c)
    silu_cc = jax.nn.silu(c_ctx)
    for i in range(DEPTH):
        mixer, slot = i % N_MIXERS, i // N_MIXERS
        ctx_out = i < DEPTH - 1
        ctx_in = ctx_out or mixer != 0
        sh1, sc1, g1, sh2, sc2, g2 = jnp.split((silu_c @ w_mod[i] + b_mod[i])[:, None, :], N_MOD, axis=-1)
        h_lat = modulate(rms_norm(x, norm_g[i, 0]), sh1, sc1)
        h_ctx = None
        if ctx_in:
            csh1, csc1, cg1, csh2, csc2, cg2 = jnp.split(silu_cc @ w_mod[i] + b_mod[i], N_MOD, axis=-1)
            h_ctx = modulate(rms_norm(ctx, norm_g[i, 0]), csh1, csc1)
        if mixer == 0:
            y_lat = fourier_mix(h_lat) @ fnet_w_out[slot]
            y_ctx = fourier_mix(h_ctx) @ fnet_w_out[slot] if ctx_out else None
        elif mixer == 1:
            y_lat, y_ctx = hgrn2_mixer(h_lat, h_ctx, hgrn_w_in[slot], lower_bounds[i], hgrn_gnorm[slot], hgrn_w_out[slot], ctx_out)
        else:
            lambda_init = 0.8 - 0.6 * math.exp(-0.3 * i)
            y_lat, y_ctx = diff_attention(h_lat, h_ctx, diff_w_qkv[slot], diff_lambda[slot], diff_subln[slot], diff_w_out[slot], lambda_init, cos, sin, ctx_out)
        x = x + g1 * rms_norm(y_lat, norm_g[i, 1])
        x = x + g2 * rms_norm(sqrelu_mlp(modulate(rms_norm(x, norm_g[i, 2]), sh2, sc2), w_mlp_in[i], w_mlp_out[i]), norm_g[i, 3])
        if ctx_out:
            ctx = ctx + cg1 * rms_norm(y_ctx, norm_g[i, 1])
            ctx = ctx + cg2 * rms_norm(sqrelu_mlp(modulate(rms_norm(ctx, norm_g[i, 2]), csh2, csc2), w_mlp_in[i], w_mlp_out[i]), norm_g[i, 3])
    return x
```

```python
import contextlib
import numpy as np
import concourse.bass as bass
import concourse.mybir as mybir
from concourse.bass_utils import run_bass_kernel_spmd

F32 = mybir.dt.float32
BF16 = mybir.dt.bfloat16
AF = mybir.ActivationFunctionType
ALU = mybir.AluOpType

ENGS = ("sp", "act", "dve", "pool", "pe")


class Buf:
    def __init__(self, ap):
        self.ap = ap
        self.w = None
        self.r = {}


class Prog:
    NDMA = 6

    def __init__(self, nc, stack):
        self.nc = nc
        self.stack = stack
        self.streams = {e: [] for e in ENGS}
        self.sem = {}
        self.cnt = {}
        self.waited = {e: {} for e in ENGS}
        for e in ("act", "dve", "pool", "pe"):
            self._mksem("c_" + e)
        self.dma_rr = {}
        for q in ("sp", "pool", "act"):
            for i in range(self.NDMA):
                self._mksem(f"d_{q}{i}")
            self.dma_rr[q] = 0
        self.n_sb = 0

    def _mksem(self, name):
        self.sem[name] = self.stack.enter_context(self.nc.semaphore(name))
        self.cnt[name] = 0

    def sb(self, shape, dt, name=None):
        self.n_sb += 1
        t = self.stack.enter_context(self.nc.sbuf_tensor(name or f"sb{self.n_sb}", list(shape), dt))
        return t

    def ps(self, shape, dt=F32, name=None):
        self.n_sb += 1
        t = self.stack.enter_context(self.nc.psum_tensor(name or f"ps{self.n_sb}", list(shape), dt))
        return t

    def _waits(self, eng, evs):
        for ev in evs:
            if ev is None:
                continue
            s, v = ev
            if eng == "pe" and s == "c_pe":
                continue
            if self.waited[eng].get(s, 0) < v:
                self.waited[eng][s] = v
                self.streams[eng].append(("wait", s, v))

    def _deps(self, reads, writes):
        evs = []
        for b in reads:
            evs.append(b.w)
        for b in writes:
            evs.append(b.w)
            for s, v in b.r.items():
                evs.append((s, v))
        return evs

    def _mark(self, ev, reads, writes):
        s, v = ev
        for b in reads:
            if b.r.get(s, 0) < v:
                b.r[s] = v
        for b in writes:
            b.w = ev
            b.r = {}

    def op(self, eng, fn, reads=(), writes=(), extra=()):
        self._waits(eng, self._deps(reads, writes) + list(extra))
        s = "c_" + eng
        self.cnt[s] += 1
        ev = (s, self.cnt[s])
        self.streams[eng].append(("op", fn, s, 1))
        self._mark(ev, reads, writes)
        return ev

    def mm(self, fns, reads=(), writes=(), extra=()):
        eng = "pe"
        self._waits(eng, self._deps(reads, writes) + list(extra))
        s = "c_pe"
        self.cnt[s] += 1
        ev = (s, self.cnt[s])
        for fn in fns[:-1]:
            self.streams[eng].append(("op", fn, None, 0))
        self.streams[eng].append(("op", fns[-1], s, 1))
        self._mark(ev, reads, writes)
        return ev

    def dma(self, q, out_ap, in_ap, reads=(), writes=(), extra=()):
        i = self.dma_rr[q]
        self.dma_rr[q] = (i + 1) % self.NDMA
        s = f"d_{q}{i}"
        prev = (s, self.cnt[s]) if self.cnt[s] > 0 else None
        self._waits(q, self._deps(reads, writes) + list(extra) + [prev])
        self.cnt[s] += 16
        ev = (s, self.cnt[s])
        self.streams[q].append(("dma", out_ap, in_ap, s))
        self._mark(ev, reads, writes)
        return ev

    def wait(self, eng, evs):
        self._waits(eng, evs)

    def emit(self, final_waits):
        nc = self.nc
        self._waits("sp", final_waits)
        with nc.Block() as block:
            def runner(ename):
                def run(engine):
                    for it in self.streams[ename]:
                        if it[0] == "wait":
                            engine.wait_ge(self.sem[it[1]], it[2])
                        elif it[0] == "op":
                            ins = it[1](engine)
                            if it[2] is not None:
                                ins.then_inc(self.sem[it[2]], it[3])
                        elif it[0] == "dma":
                            engine.dma_start(out=it[1], in_=it[2]).then_inc(self.sem[it[3]], 16)
                return run
            block.sync(runner("sp"))
            block.scalar(runner("act"))
            block.vector(runner("dve"))
            block.gpsimd(runner("pool"))
            block.tensor(runner("pe"))


T = 1152
TLAT = 1024
D = 2048
KC = 16
TT = 384
NTT = 3
HID = 8192
EPS = 1e-6
MODW = 96


class Ctx:
    pass


def alias_from(dst, src):
    dst.w = src.w
    dst.r = dict(src.r)


def merge_into(dst, src):
    evs = list(src.r.items())
    if src.w is not None:
        evs.append(src.w)
    for s_, v_ in evs:
        if dst.r.get(s_, 0) < v_:
            dst.r[s_] = v_


def setup_common(P):
    c = Ctx()
    c.P = P
    c.banks = [Buf(P.ps([128, 512], F32)) for _ in range(8)]
    c.bank_rr = 0
    ones = P.sb([128, 128], BF16)
    c.ones = Buf(ones)
    P.op("dve", lambda e: e.memset(ones[:, :], 1.0), writes=[c.ones])
    epst = P.sb([128, 1], F32)
    c.eps = Buf(epst)
    P.op("dve", lambda e: e.memset(epst[:, :], EPS), writes=[c.eps])
    return c


def next_bank(c, lo=0, hi=8):
    n = hi - lo
    b = c.banks[lo + (c.bank_rr % n)]
    c.bank_rr += 1
    return b


def rms_rstd(c, src, srcb, rstd, rstdb, sq, ncol=T, dim=D, nch=KC):
    P = c.P
    tts = [(a, min(a + TT, ncol)) for a in range(0, ncol, TT)]
    pbs = [c.banks[i] for i in range(len(tts))]
    for ch in range(nch):
        sb = sq[ch % 2]
        P.op("act", lambda e, ch=ch, sb=sb: e.activation(sb.ap[:, 0:ncol], src[:, ch, 0:ncol], AF.Square),
             reads=[srcb], writes=[sb])
        fns = []
        for i, (a, b) in enumerate(tts):
            fns.append(lambda pe, i=i, a=a, b=b, sb=sb, ch=ch: pe.matmul(
                pbs[i].ap[:, 0:b - a], c.ones.ap[:, :], sb.ap[:, a:b], start=(ch == 0), stop=(ch == nch - 1)))
        if ch == 0 or ch == nch - 1:
            P.mm(fns, reads=[sb, c.ones], writes=pbs)
        else:
            P.mm(fns, reads=[sb, c.ones])
    for i, (a, b) in enumerate(tts):
        P.op("act", lambda e, i=i, a=a, b=b: e.activation(rstd[:, a:b], pbs[i].ap[:, 0:b - a], AF.Sqrt,
                                                         bias=c.eps.ap[:, 0:1], scale=1.0 / dim),
             reads=[pbs[i], c.eps], writes=[rstdb])
    P.op("dve", lambda e: e.reciprocal(rstd[:, 0:ncol], rstd[:, 0:ncol]), reads=[rstdb], writes=[rstdb])


def gemm_stream(c, w_ap, nk, ncols, wbufs, rhs_fn, rhs_bufs, evac, tts, banks=(0, 8), wq="pool"):
    P = c.P
    nblk = ncols // 512
    wv = w_ap.rearrange("(kc p) m -> p kc m", p=128)
    for j in range(nblk):
        wb = wbufs[j % len(wbufs)]
        wt = wb.ap.rearrange("p (kc m) -> p kc m", kc=nk)
        P.dma(wq, wt, wv[:, :, j * 512:(j + 1) * 512], writes=[wb])
        for m in range(4):
            for ti, (a, b) in enumerate(tts):
                pb = next_bank(c, *banks)
                fns = [(lambda pe, kc=kc, m=m, a=a, b=b, pb=pb, wt=wt: pe.matmul(
                    pb.ap[:, 0:b - a], wt[:, kc, m * 128:(m + 1) * 128], rhs_fn(kc, a, b),
                    start=(kc == 0), stop=(kc == nk - 1))) for kc in range(nk)]
                P.mm(fns, reads=[wb] + list(rhs_bufs), writes=[pb])
                evac(j * 4 + m, ti, a, b, pb)


def build_rowstage(nc, do_c, do_a, names):
    dram = {}

    def din(name, shape, dt):
        dram[name] = nc.dram_tensor(name, list(shape), dt, kind="ExternalInput").ap()
        return dram[name]

    def dout(name, shape, dt):
        dram[name] = nc.dram_tensor(name, list(shape), dt, kind="ExternalOutput").ap()
        return dram[name]

    xT = din("xT", [D, T], F32)
    nl = int(do_c) + int(do_a)
    modd = din("modc", [128, nl * MODW * 2], F32)
    gnd = din("gn", [128, nl * 64], F32)
    if do_c:
        ymd = din("ym", [D, T], BF16)
        woutd = din("wout", [D, D], F32)
        w1d = din("w1", [D, HID], F32)
        w2d = din("w2", [HID, D], F32)
        xo = dout("xo", [D, T], F32)
        xpark = nc.dram_tensor("xpark", [D, T], F32, kind="Internal").ap()
    if do_a:
        ho = dout("ho", [D, T], BF16)

    with contextlib.ExitStack() as st:
        P = Prog(nc, st)
        c = setup_common(P)
        R = P.sb([128, KC * T], F32, "R")
        Rb = Buf(R)
        Rx = R[:, :].rearrange("p (c t) -> p c t", c=KC)
        HBt = P.sb([128, KC * T], BF16, "HB")
        HB = Buf(HBt)
        HBv = HBt[:, :].rearrange("p (c t) -> p c t", c=KC)
        modt = P.sb([128, nl * MODW * 2], F32, "modt")
        modb = Buf(modt)
        modv = modt[:, :].rearrange("p (l w k) -> p l w k", l=nl, k=2)
        gnt = P.sb([128, nl * 64], F32, "gnt")
        gnb = Buf(gnt)
        gnv = gnt[:, :].rearrange("p (l j c) -> p l j c", l=nl, j=4)
        rstdt = P.sb([128, T], F32, "rstd")
        rstdb = Buf(rstdt)
        coeft = P.sb([128, 4 * KC * 2], F32, "coef")
        coefb = Buf(coeft)
        coefv = coeft[:, :].rearrange("p (j c k) -> p j c k", j=4, k=2)
        P.dma("sp", modt[:, :], modd, writes=[modb])
        P.dma("sp", gnt[:, :], gnd, writes=[gnb])
        xv = xT.rearrange("(c p) t -> p c t", p=128)
        finals = []

        def mk_ab(li, sc_part, g_idx, slot):
            for k in range(2):
                P.op("dve", lambda e, k=k: e.scalar_tensor_tensor(
                    coefv[:, slot, :, k], modv[:, li, sc_part * 16:(sc_part + 1) * 16, k], 1.0,
                    gnv[:, li, g_idx, :], ALU.add, ALU.mult), reads=[modb, gnb], writes=[coefb])

        def mk_gate(li, g_part, g_idx, slot):
            for k in range(2):
                P.op("dve", lambda e, k=k: e.tensor_tensor(
                    coefv[:, slot, :, k], modv[:, li, g_part * 16:(g_part + 1) * 16, k],
                    gnv[:, li, g_idx, :], ALU.mult), reads=[modb, gnb], writes=[coefb])

        def modulate(srcv, srcb, tmpb_list, li, slot, sh_part, dstv, dstb):
            for ch in range(KC):
                tb = tmpb_list[ch % 2]
                P.op("dve", lambda e, ch=ch, tb=tb: e.tensor_tensor(tb.ap[:, :], srcv[:, ch, :], rstdt[:, :], ALU.mult),
                     reads=[srcb, rstdb], writes=[tb])
                for k, (a, b) in enumerate(((0, TLAT), (TLAT, T))):
                    P.op("act", lambda e, ch=ch, tb=tb, k=k, a=a, b=b: e.activation(
                        dstv[:, ch, a:b], tb.ap[:, a:b], AF.Identity,
                        bias=modv[:, li, sh_part * 16 + ch, k:k + 1], scale=coefv[:, slot, ch, k:k + 1]),
                         reads=[tb, modb, coefb], writes=[dstb])

        if do_c:
            Yt = P.sb([128, KC * T], F32, "Y")
            Yb = Buf(Yt)
            Yv = Yt[:, :].rearrange("p (c t) -> p c t", c=KC)
            hidt = [P.sb([128, 4 * T], BF16, f"hid{i}") for i in range(2)]
            hidb = [Buf(t) for t in hidt]
            Rbf = R[:, :].bitcast(BF16)
            wtiles = [Buf(Rbf[:, i * 8192:(i + 1) * 8192]) for i in range(4)]
            sqs = [Buf(Rbf[:, 4 * 8192 + i * T: 4 * 8192 + (i + 1) * T]) for i in range(2)]
            tmpY = Buf(Yt[:, 0:T])
            tts = [(i * TT, (i + 1) * TT) for i in range(NTT)]
            P.dma("sp", HBv, ymd.rearrange("(c p) t -> p c t", p=128), writes=[HB])

            def evac_y(mc, ti, a, b, pb):
                P.op("act", lambda e: e.activation(Yv[:, mc, a:b], pb.ap[:, 0:b - a], AF.Copy),
                     reads=[pb], writes=[Yb])
            gemm_stream(c, woutd, KC, D, wtiles[0:2], lambda kc, a, b: HBv[:, kc, a:b], [HB], evac_y, tts)
            mk_gate(0, 2, 1, 0)
            mk_ab(0, 4, 2, 1)
            mk_gate(0, 5, 3, 2)
            rms_rstd(c, Yv, Yb, rstdt, rstdb, sqs)
            P.dma("sp", Rx, xv, reads=[], writes=[Rb] + wtiles[0:2] + sqs)

            def resid(srcv, srcb, slot):
                for ch in range(KC):
                    P.op("dve", lambda e, ch=ch: e.tensor_tensor(srcv[:, ch, :], srcv[:, ch, :], rstdt[:, :], ALU.mult),
                         reads=[rstdb], writes=[srcb])
                    for k, (a, b) in enumerate(((0, TLAT), (TLAT, T))):
                        P.op("dve", lambda e, ch=ch, k=k, a=a, b=b: e.scalar_tensor_tensor(
                            Rx[:, ch, a:b], srcv[:, ch, a:b], coefv[:, slot, ch, k:k + 1], Rx[:, ch, a:b],
                            ALU.mult, ALU.add), reads=[srcb, coefb], writes=[Rb])
            resid(Yv, Yb, 0)
            sq2 = [Buf(hidt[i][:, 0:T]) for i in range(2)]
            for i in range(2):
                sq2[i].w = hidb[i].w
            rms_rstd(c, Rx, Rb, rstdt, rstdb, sq2)
            alias_from(tmpY, Yb)
            merge_into(tmpY, Yb)
            modulate(Rx, Rb, [tmpY, tmpY], 0, 1, 3, HBv, HB)
            merge_into(Yb, tmpY)
            P.dma("sp", xpark.rearrange("(c p) t -> p c t", p=128), Rx, reads=[Rb])
            for wt_ in wtiles:
                alias_from(wt_, Rb)
            for i in range(2):
                hidb[i].w = sq2[i].w
                hidb[i].r = dict(sq2[i].r)
            NG = HID // 512
            w1v = w1d.rearrange("(kc p) m -> p kc m", p=128)

            def phaseA(g):
                wb = wtiles[g % 2]
                wt = wb.ap.rearrange("p (kc m) -> p kc m", kc=KC)
                P.dma("pool", wt, w1v[:, :, g * 512:(g + 1) * 512], writes=[wb])
                hb = hidb[g % 2]
                hv = hidt[g % 2][:, :].rearrange("p (m t) -> p m t", m=4)
                for m in range(4):
                    for (a, b) in tts:
                        pb = next_bank(c, 0, 4)
                        fns = [(lambda pe, kc=kc, m=m, a=a, b=b, pb=pb, wt=wt: pe.matmul(
                            pb.ap[:, 0:b - a], wt[:, kc, m * 128:(m + 1) * 128], HBv[:, kc, a:b],
                            start=(kc == 0), stop=(kc == KC - 1))) for kc in range(KC)]
                        P.mm(fns, reads=[wb, HB], writes=[pb])
                        P.op("act", lambda e, pb=pb, a=a, b=b: e.activation(pb.ap[:, 0:b - a], pb.ap[:, 0:b - a], AF.Relu),
                             reads=[pb], writes=[pb])
                        P.op("act", lambda e, pb=pb, a=a, b=b, m=m, hv=hv: e.activation(hv[:, m, a:b], pb.ap[:, 0:b - a], AF.Square),
                             reads=[pb], writes=[hb])

            def phaseB(g):
                wb = wtiles[2 + g % 2]
                wt = wb.ap.rearrange("p (kc m) -> p kc m", kc=4)
                P.dma("pool", wt, w2d[g * 512:(g + 1) * 512, :].rearrange("(kc p) m -> p kc m", p=128), writes=[wb])
                hb = hidb[g % 2]
                hv = hidt[g % 2][:, :].rearrange("p (m t) -> p m t", m=4)
                for mo in range(KC):
                    for (a, b) in tts:
                        pb = next_bank(c, 4, 8)
                        fns = [(lambda pe, kc=kc, mo=mo, a=a, b=b, pb=pb, wt=wt, hv=hv: pe.matmul(
                            pb.ap[:, 0:b - a], wt[:, kc, mo * 128:(mo + 1) * 128], hv[:, kc, a:b],
                            start=(kc == 0), stop=(kc == 3))) for kc in range(4)]
                        P.mm(fns, reads=[wb, hb], writes=[pb])
                        if g == 0:
                            P.op("dve", lambda e, pb=pb, mo=mo, a=a, b=b: e.tensor_copy(Yv[:, mo, a:b], pb.ap[:, 0:b - a]),
                                 reads=[pb], writes=[Yb])
                        else:
                            P.op("dve", lambda e, pb=pb, mo=mo, a=a, b=b: e.tensor_tensor(
                                Yv[:, mo, a:b], pb.ap[:, 0:b - a], Yv[:, mo, a:b], ALU.add),
                                 reads=[pb], writes=[Yb])
            phaseA(0)
            for g in range(NG):
                if g + 1 < NG:
                    phaseA(g + 1)
                phaseB(g)
            sq3 = [Buf(hidt[i][:, 0:T]) for i in range(2)]
            for i in range(2):
                sq3[i].w = hidb[i].w
                sq3[i].r = dict(hidb[i].r)
            rms_rstd(c, Yv, Yb, rstdt, rstdb, sq3)
            P.dma("sp", Rx, xpark.rearrange("(c p) t -> p c t", p=128), writes=[Rb] + wtiles)
            resid(Yv, Yb, 2)
            finals.append(P.dma("sp", xo.rearrange("(c p) t -> p c t", p=128), Rx, reads=[Rb]))
            sqa = sq3
            tmpa = tmpY
            alias_from(tmpY, Yb)
            merge_into(tmpY, Yb)
        else:
            P.dma("sp", Rx, xv, writes=[Rb])
            sqa = [Buf(P.sb([128, T], BF16, f"sqa{i}")) for i in range(2)]
            tmpa = Buf(P.sb([128, T], F32, "tmpa"))
        if do_a:
            la = nl - 1
            mk_ab(la, 1, 0, 3)
            rms_rstd(c, Rx, Rb, rstdt, rstdb, sqa)
            modulate(Rx, Rb, [tmpa, tmpa], la, 3, 0, HBv, HB)
            finals.append(P.dma("sp", ho.rearrange("(c p) t -> p c t", p=128), HBv, reads=[HB]))
        P.emit(finals)
    return nc


def build_fnet(nc):
    hmy = nc.dram_tensor("hmy", [2, 1024, T], BF16, kind="ExternalInput").ap()
    cs256 = nc.dram_tensor("cs256", [128, 2 * 512], BF16, kind="ExternalInput").ap()
    dftL = nc.dram_tensor("dftL", [2, 2048, 2048], BF16, kind="ExternalInput").ap()
    dftC = nc.dram_tensor("dftC", [128, 2 * 2 * 256], BF16, kind="ExternalInput").ap()
    ymo = nc.dram_tensor("ymo", [2, 1024, T], BF16, kind="ExternalOutput").ap()
    with contextlib.ExitStack() as st:
        P = Prog(nc, st)
        c = setup_common(P)
        hst = P.sb([128, 8 * 2 * T], BF16, "hs")
        hsb = Buf(hst)
        hs = hst[:, :].rearrange("p (f r t) -> p f r t", f=8, r=2)
        ABt = P.sb([128, 18 * 4 * 512], BF16, "AB")
        ABb = Buf(ABt)
        AB = ABt[:, :].rearrange("p (b g n) -> p b g n", b=18, g=4)
        cst = P.sb([128, 2 * 512], BF16, "cs")
        csb = Buf(cst)
        cs = cst[:, :].rearrange("p (k n) -> p k n", k=2)
        dct = P.sb([128, 2 * 2 * 256], BF16, "dc")
        dcb = Buf(dct)
        dc = dct[:, :].rearrange("p (a k n) -> p a k n", a=2, k=2)
        dl = [[Buf(P.sb([128, 16 * 512], BF16, f"dl{i}{a}")) for a in range(2)] for i in range(2)]
        P.dma("sp", cst[:, :], cs256, writes=[csb])
        P.dma("sp", dct[:, :], dftC, writes=[dcb])
        for r in range(2):
            P.dma("sp", hs[:, :, r, :], hmy[r].rearrange("(f p) t -> p f t", p=128), writes=[hsb])
        blocks = [(lb // 8, (lb % 8) * 128) for lb in range(16)] + [(0, TLAT), (1, TLAT)]
        n = 0
        for bi, (r, c0) in enumerate(blocks):
            for gl in range(4):
                pb = next_bank(c)
                fns = [(lambda pe, kc=kc, gl=gl, r=r, c0=c0, pb=pb: pe.matmul(
                    pb.ap[:, :], hs[:, 2 * gl + kc, r, c0:c0 + 128], cs[:, kc, :],
                    start=(kc == 0), stop=(kc == 1))) for kc in range(2)]
                P.mm(fns, reads=[hsb, csb], writes=[pb])
                if n % 2 == 0:
                    P.op("act", lambda e, bi=bi, gl=gl, pb=pb: e.activation(AB[:, bi, gl, :], pb.ap[:, :], AF.Copy),
                         reads=[pb], writes=[ABb])
                else:
                    P.op("dve", lambda e, bi=bi, gl=gl, pb=pb: e.tensor_copy(AB[:, bi, gl, :], pb.ap[:, :]),
                         reads=[pb], writes=[ABb])
                n += 1
        for lt in range(4):
            tl = dl[lt % 2]
            tv = [tl[a].ap.rearrange("p (b n) -> p b n", b=16) for a in range(2)]
            for a in range(2):
                P.dma("sp", tv[a], dftL[a].rearrange("(b p) n -> p b n", p=128)[:, :, lt * 512:(lt + 1) * 512],
                      writes=[tl[a]])
            r, t0 = lt // 2, (lt % 2) * 512
            for fc in range(8):
                gl, off = fc // 2, (fc % 2) * 128
                pb = next_bank(c)
                fns = []
                for lb in range(16):
                    for a in range(2):
                        fns.append(lambda pe, lb=lb, a=a, gl=gl, off=off, pb=pb, tv=tv: pe.matmul(
                            pb.ap[:, :], AB[:, lb, gl, a * 256 + off:a * 256 + off + 128], tv[a][:, lb, :],
                            start=(lb == 0 and a == 0), stop=(lb == 15 and a == 1)))
                P.mm(fns, reads=[ABb, tl[0], tl[1]], writes=[pb])
                if fc % 2 == 0:
                    P.op("act", lambda e, fc=fc, r=r, t0=t0, pb=pb: e.activation(hs[:, fc, r, t0:t0 + 512], pb.ap[:, :], AF.Copy),
                         reads=[pb], writes=[hsb])
                else:
                    P.op("dve", lambda e, fc=fc, r=r, t0=t0, pb=pb: e.tensor_copy(hs[:, fc, r, t0:t0 + 512], pb.ap[:, :]),
                         reads=[pb], writes=[hsb])
        for fc in range(8):
            gl, off = fc // 2, (fc % 2) * 128
            pb = next_bank(c)
            fns = []
            for cb in range(2):
                for a in range(2):
                    fns.append(lambda pe, cb=cb, a=a, gl=gl, off=off, pb=pb: pe.matmul(
                        pb.ap[:, 0:256], AB[:, 16 + cb, gl, a * 256 + off:a * 256 + off + 128], dc[:, a, cb, :],
                        start=(cb == 0 and a == 0), stop=(cb == 1 and a == 1)))
            P.mm(fns, reads=[ABb, dcb], writes=[pb])
            for r in range(2):
                P.op("act", lambda e, fc=fc, r=r, pb=pb: e.activation(hs[:, fc, r, TLAT:T], pb.ap[:, r * 128:(r + 1) * 128], AF.Copy),
                     reads=[pb], writes=[hsb])
        fin = []
        for r in range(2):
            fin.append(P.dma("sp", ymo[r].rearrange("(f p) t -> p f t", p=128), hs[:, :, r, :], reads=[hsb]))
        P.emit(fin)
    return nc


def fnet_tables():
    import ml_dtypes
    bf = ml_dtypes.bfloat16
    def cs(n):
        k = np.arange(n)
        ang = 2.0 * np.pi * ((k[:, None] * k[None, :]) % n) / n
        return np.cos(ang), np.sin(ang)
    c256, s256 = cs(256)
    cL, sL = cs(2048)
    t = np.concatenate([c256, s256], axis=1).reshape(2, 128, 512).transpose(1, 0, 2).reshape(128, 1024)
    dftL = np.stack([cL, -sL]).astype(bf)
    dC = np.stack([c256, -s256]).reshape(2, 2, 128, 256).transpose(2, 0, 1, 3).reshape(128, 1024)
    return {"cs256": np.ascontiguousarray(t.astype(bf)), "dftL": np.ascontiguousarray(dftL),
            "dftC": np.ascontiguousarray(dC.astype(bf))}


def build_attn(nc, lambda_init):
    hall = nc.dram_tensor("hall", [2, D, T], BF16, kind="ExternalInput").ap()
    wqkv = nc.dram_tensor("wqkv", [3, D, 1024], F32, kind="ExternalInput").ap()
    ropeC = nc.dram_tensor("ropeC", [128, 2048], F32, kind="ExternalInput").ap()
    ropeS = nc.dram_tensor("ropeS", [128, 2048], F32, kind="ExternalInput").ap()
    pswapd = nc.dram_tensor("pswap", [128, 128], F32, kind="ExternalInput").ap()
    lamTd = nc.dram_tensor("lamT", [128, 4], F32, kind="ExternalInput").ap()
    sublnd = nc.dram_tensor("subln", [128, 2], F32, kind="ExternalInput").ap()
    ymo = nc.dram_tensor("ymo", [2, 1024, T], BF16, kind="ExternalOutput").ap()
    NTOK = 2304
    SCALE = 128 ** -0.5
    with contextlib.ExitStack() as st:
        P = Prog(nc, st)
        c = setup_common(P)
        ht = P.sb([128, KC * 2 * T], BF16, "hall_sb")
        hb = Buf(ht)
        hv = ht[:, :].rearrange("p (c r t) -> p c r t", c=KC, r=2)
        for r in range(2):
            P.dma("sp", hv[:, :, r, :], hall[r].rearrange("(c p) t -> p c t", p=128), writes=[hb])
        rCt = P.sb([128, 2048], F32, "rC"); rCb = Buf(rCt)
        rSt = P.sb([128, 2048], F32, "rS"); rSb = Buf(rSt)
        pswt = P.sb([128, 128], F32, "psw"); pswb = Buf(pswt)
        lamt = P.sb([128, 4], F32, "lamt_sb"); lamb = Buf(lamt)
        slt = P.sb([128, 2], F32, "slt_sb"); slb = Buf(slt)
        P.dma("sp", rCt[:, :], ropeC, writes=[rCb])
        P.dma("sp", rSt[:, :], ropeS, writes=[rSb])
        P.dma("sp", pswt[:, :], pswapd, writes=[pswb])
        P.dma("sp", lamt[:, :], lamTd, writes=[lamb])
        P.dma("sp", slt[:, :], sublnd, writes=[slb])
        onesf = P.sb([128, 128], F32, "onesf"); onesfb = Buf(onesf)
        P.op("dve", lambda e: e.memset(onesf[:, :], 1.0), writes=[onesfb])
        lw = P.sb([128, 8], F32, "lw"); lwb = Buf(lw)
        P.op("dve", lambda e: e.tensor_tensor(lw[:, 0:1], lamt[:, 0:1], lamt[:, 1:2], ALU.mult), reads=[lamb], writes=[lwb])
        P.op("dve", lambda e: e.tensor_tensor(lw[:, 1:2], lamt[:, 2:3], lamt[:, 3:4], ALU.mult), reads=[lamb], writes=[lwb])
        pb = next_bank(c)
        P.mm([lambda pe, pb=pb: pe.matmul(pb.ap[:, 0:2], onesf[:, :], lw[:, 0:2], start=True, stop=True)],
             reads=[onesfb, lwb], writes=[pb])
        P.op("act", lambda e, pb=pb: e.activation(lw[:, 2:4], pb.ap[:, 0:2], AF.Exp), reads=[pb], writes=[lwb])
        P.op("dve", lambda e: e.tensor_tensor(lw[:, 4:5], lw[:, 2:3], lw[:, 3:4], ALU.subtract), reads=[lwb], writes=[lwb])
        P.op("dve", lambda e: e.tensor_scalar(lw[:, 5:6], lw[:, 4:5], float(lambda_init), -1.0, ALU.add, ALU.mult),
             reads=[lwb], writes=[lwb])
        P.op("dve", lambda e: e.tensor_scalar(slt[:, :], slt[:, :], float(1.0 - lambda_init), None, ALU.mult),
             reads=[slb], writes=[slb])
        QKt = P.sb([128, 4 * NTOK], BF16, "QK"); QKb = Buf(QKt)
        QK = QKt[:, :].rearrange("p (a n) -> p a n", a=4)
        Vt = P.sb([128, 18 * 256], BF16, "V"); Vb = Buf(Vt)
        V = Vt[:, :].rearrange("p (b n) -> p b n", b=18)
        wts = [Buf(P.sb([128, KC * 256], BF16, f"wt{i}")) for i in range(2)]
        oat = P.sb([128, 2 * NTOK], F32, "oacc"); oab = Buf(oat)
        oa = oat[:, :].rearrange("p (v n) -> p v n", v=2)
        rstdt = P.sb([128, NTOK], F32, "rstd"); rstdb = Buf(rstdt)
        sqs = [Buf(P.sb([128, NTOK], BF16, f"sq{i}")) for i in range(2)]
        pts = [Buf(P.sb([128, 512], BF16, f"pt{i}")) for i in range(3)]
        xf = [Buf(P.sb([128, 512], F32, f"xf{i}")) for i in range(2)]
        t1 = Buf(P.sb([128, 512], F32, "t1"))
        t2 = Buf(P.sb([128, 512], F32, "t2"))
        rz = Buf(P.sb([128, 512], F32, "rz"))
        yst = [Buf(P.sb([128, 2 * NTOK], BF16, f"yst{i}")) for i in range(2)]
        widx = 0
        ltiles = [(r, a, 512, r * 1024 + a, True) for r in range(2) for a in (0, 512)]
        ctiles = [(r, TLAT, 128, 2048 + r * 128, False) for r in range(2)]
        blocks = [(lb // 8, (lb % 8) * 128) for lb in range(16)] + [(0, TLAT), (1, TLAT)]
        fin = []
        for hd in range(4):
            for qk in range(2):
                wb = wts[widx % 2]; widx += 1
                wt = wb.ap.rearrange("p (kc m) -> p kc m", kc=KC)
                P.dma("pool", wt, wqkv[qk].rearrange("(kc p) m -> p kc m", p=128)[:, :, hd * 256:(hd + 1) * 256], writes=[wb])
                for sub in range(2):
                    dst = qk * 2 + sub
                    for (r, a, n, sc0, lat) in ltiles + ctiles:
                        pb = next_bank(c)
                        fns = [(lambda pe, kc=kc, sub=sub, r=r, a=a, n=n, pb=pb, wt=wt: pe.matmul(
                            pb.ap[:, 0:n], wt[:, kc, sub * 128:(sub + 1) * 128], hv[:, kc, r, a:a + n],
                            start=(kc == 0), stop=(kc == KC - 1))) for kc in range(KC)]
                        P.mm(fns, reads=[wb, hb], writes=[pb])
                        if not lat:
                            P.op("act", lambda e, dst=dst, sc0=sc0, n=n, pb=pb: e.activation(QK[:, dst, sc0:sc0 + n], pb.ap[:, 0:n], AF.Copy),
                                 reads=[pb], writes=[QKb])
                            continue
                        xb = xf[(sc0 // 512) % 2]
                        P.op("act", lambda e, xb=xb, pb=pb: e.activation(xb.ap[:, :], pb.ap[:, :], AF.Copy), reads=[pb], writes=[xb])
                        pb2 = next_bank(c)
                        P.mm([lambda pe, pb2=pb2, xb=xb: pe.matmul(pb2.ap[:, :], pswt[:, :], xb.ap[:, :], start=True, stop=True)],
                             reads=[pswb, xb], writes=[pb2])
                        P.op("dve", lambda e, xb=xb, sc0=sc0: e.tensor_tensor(t1.ap[:, :], xb.ap[:, :], rCt[:, sc0:sc0 + 512], ALU.mult),
                             reads=[xb, rCb], writes=[t1])
                        P.op("dve", lambda e, pb2=pb2, sc0=sc0: e.tensor_tensor(t2.ap[:, :], pb2.ap[:, :], rSt[:, sc0:sc0 + 512], ALU.mult),
                             reads=[pb2, rSb], writes=[t2])
                        P.op("dve", lambda e, dst=dst, sc0=sc0: e.tensor_tensor(QK[:, dst, sc0:sc0 + 512], t1.ap[:, :], t2.ap[:, :], ALU.add),
                             reads=[t1, t2], writes=[QKb])
            wb = wts[widx % 2]; widx += 1
            wt = wb.ap.rearrange("p (kc m) -> p kc m", kc=KC)
            P.dma("pool", wt, wqkv[2].rearrange("(kc p) m -> p kc m", p=128)[:, :, hd * 256:(hd + 1) * 256], writes=[wb])
            for bi, (r, c0) in enumerate(blocks):
                pb = next_bank(c)
                fns = [(lambda pe, kc=kc, r=r, c0=c0, pb=pb, wt=wt: pe.matmul(
                    pb.ap[:, 0:256], hv[:, kc, r, c0:c0 + 128], wt[:, kc, :],
                    start=(kc == 0), stop=(kc == KC - 1))) for kc in range(KC)]
                P.mm(fns, reads=[wb, hb], writes=[pb])
                P.op("act", lambda e, bi=bi, pb=pb: e.activation(V[:, bi, :], pb.ap[:, 0:256], AF.Copy), reads=[pb], writes=[Vb])
            qtiles = [(q0, 512, list(range(18))) for q0 in range(0, 2048, 512)] + [(2048, 256, [16, 17])]
            for (q0, nq, kbs) in qtiles:
                for sub in range(2):
                    acc = [c.banks[2 + sub * 3 + i] for i in range(3)]
                    for ki, kb in enumerate(kbs):
                        sbk = c.banks[ki % 2]
                        P.mm([lambda pe, sbk=sbk, sub=sub, kb=kb, q0=q0, nq=nq: pe.matmul(
                            sbk.ap[:, 0:nq], QK[:, 2 + sub, kb * 128:(kb + 1) * 128], QK[:, sub, q0:q0 + nq],
                            start=True, stop=True)], reads=[QKb], writes=[sbk])
                        pt = pts[ki % 3]
                        P.op("act", lambda e, pt=pt, sbk=sbk, nq=nq: e.activation(pt.ap[:, 0:nq], sbk.ap[:, 0:nq], AF.Exp, scale=SCALE),
                             reads=[sbk], writes=[pt])
                        first, last = (ki == 0), (ki == len(kbs) - 1)
                        fns = [
                            lambda pe, pt=pt, kb=kb, nq=nq, first=first, last=last, acc=acc: pe.matmul(
                                acc[0].ap[:, 0:nq], V[:, kb, 0:128], pt.ap[:, 0:nq], start=first, stop=last),
                            lambda pe, pt=pt, kb=kb, nq=nq, first=first, last=last, acc=acc: pe.matmul(
                                acc[1].ap[:, 0:nq], V[:, kb, 128:256], pt.ap[:, 0:nq], start=first, stop=last),
                            lambda pe, pt=pt, nq=nq, first=first, last=last, acc=acc: pe.matmul(
                                acc[2].ap[:, 0:nq], c.ones.ap[:, :], pt.ap[:, 0:nq], start=first, stop=last),
                        ]
                        if first or last:
                            P.mm(fns, reads=[pt, Vb, c.ones], writes=acc)
                        else:
                            P.mm(fns, reads=[pt, Vb, c.ones])
                    P.op("dve", lambda e, acc=acc, nq=nq: e.reciprocal(rz.ap[:, 0:nq], acc[2].ap[:, 0:nq]), reads=[acc[2]], writes=[rz])
                    for vc in range(2):
                        if sub == 0:
                            P.op("dve", lambda e, acc=acc, vc=vc, q0=q0, nq=nq: e.tensor_tensor(
                                oa[:, vc, q0:q0 + nq], acc[vc].ap[:, 0:nq], rz.ap[:, 0:nq], ALU.mult),
                                 reads=[acc[vc], rz], writes=[oab])
                        else:
                            P.op("dve", lambda e, acc=acc, vc=vc, nq=nq: e.tensor_tensor(
                                t1.ap[:, 0:nq], acc[vc].ap[:, 0:nq], rz.ap[:, 0:nq], ALU.mult),
                                 reads=[acc[vc], rz], writes=[t1])
                            P.op("dve", lambda e, vc=vc, q0=q0, nq=nq: e.scalar_tensor_tensor(
                                oa[:, vc, q0:q0 + nq], t1.ap[:, 0:nq], lw[:, 5:6], oa[:, vc, q0:q0 + nq], ALU.mult, ALU.add),
                                 reads=[t1, lwb], writes=[oab])
            rms_rstd(c, oa, oab, rstdt, rstdb, sqs, ncol=NTOK, dim=256, nch=2)
            yb = yst[hd % 2]
            yv = yb.ap.rearrange("p (v n) -> p v n", v=2)
            for vc in range(2):
                P.op("dve", lambda e, vc=vc: e.tensor_tensor(oa[:, vc, :], oa[:, vc, :], rstdt[:, :], ALU.mult),
                     reads=[rstdb], writes=[oab])
                P.op("act", lambda e, vc=vc, yv=yv: e.activation(yv[:, vc, :], oa[:, vc, :], AF.Copy, scale=slt[:, vc:vc + 1]),
                     reads=[oab, slb], writes=[yb])
            for r in range(2):
                dsth = ymo[r, hd * 256:(hd + 1) * 256, :].rearrange("(v p) t -> p v t", p=128)
                fin.append(P.dma("sp", dsth[:, :, 0:TLAT], yv[:, :, r * 1024:(r + 1) * 1024], reads=[yb]))
                fin.append(P.dma("sp", dsth[:, :, TLAT:T], yv[:, :, 2048 + r * 128:2048 + (r + 1) * 128], reads=[yb]))
        P.emit(fin)
    return nc


def rope_tables():
    L = 2048
    row = (np.arange(L) // 64).astype(np.float32)
    col = (np.arange(L) % 64).astype(np.float32)
    inv = (10000.0 ** (-np.arange(32, dtype=np.float32) / 32)).astype(np.float32)
    ang = np.concatenate([row[:, None] * inv, col[:, None] * inv], axis=-1)
    cos, sin = np.cos(ang).T, np.sin(ang).T
    C = np.concatenate([cos, cos], 0).astype(np.float32)
    S = np.concatenate([-sin, sin], 0).astype(np.float32)
    psw = np.zeros((128, 128), np.float32)
    for m in range(128):
        psw[(m + 64) % 128, m] = 1.0
    return {"ropeC": np.ascontiguousarray(C), "ropeS": np.ascontiguousarray(S), "pswap": psw}


def build_hgrn(nc, layer_idx):
    hall = nc.dram_tensor("hall", [2, D, T], BF16, kind="ExternalInput").ap()
    win = nc.dram_tensor("win", [5, D, 1024], F32, kind="ExternalInput").ap()
    lbpd = nc.dram_tensor("lbp", [128, 4 * 2 * 8], F32, kind="ExternalInput").ap()
    gnd = nc.dram_tensor("gnorm", [128, 1], F32, kind="ExternalInput").ap()
    identd = nc.dram_tensor("ident", [128, 128], BF16, kind="ExternalInput").ap()
    maskd = nc.dram_tensor("masks", [128, 2 * 128], F32, kind="ExternalInput").ap()
    ymo = nc.dram_tensor("ymo", [2, 1024, T], BF16, kind="ExternalOutput").ap()
    NTOK = 2304
    CH = 64
    NCH = NTOK // CH
    with contextlib.ExitStack() as st:
        P = Prog(nc, st)
        c = setup_common(P)
        ht = P.sb([128, KC * 2 * T], BF16, "hall_sb")
        hb = Buf(ht)
        hv = ht[:, :].rearrange("p (c r t) -> p c r t", c=KC, r=2)
        for r in range(2):
            P.dma("sp", hv[:, :, r, :], hall[r].rearrange("(c p) t -> p c t", p=128), writes=[hb])
        lbt = P.sb([128, 64], F32, "lbt"); lbb = Buf(lbt)
        gnt = P.sb([128, 1], F32, "gnt"); gnb = Buf(gnt)
        idt = P.sb([128, 128], BF16, "idt"); idb = Buf(idt)
        mkt = P.sb([128, 256], F32, "mkt"); mkb = Buf(mkt)
        P.dma("sp", lbt[:, :], lbpd, writes=[lbb])
        P.dma("sp", gnt[:, :], gnd, writes=[gnb])
        P.dma("sp", idt[:, :], identd, writes=[idb])
        P.dma("sp", mkt[:, :], maskd, writes=[mkb])
        onesf = P.sb([128, CH], F32, "onesf"); onesfb = Buf(onesf)
        P.op("dve", lambda e: e.memset(onesf[:, :], 1.0), writes=[onesfb])
        lbv = lbt[:, :].rearrange("p (l x) -> p l x", l=4)
        lw = P.sb([128, 3 * 16], F32, "lw"); lwb = Buf(lw)
        lwv = lw[:, :].rearrange("p (a x) -> p a x", a=3)
        P.op("act", lambda e: e.activation(lbt[:, :], lbt[:, :], AF.Exp), reads=[lbb], writes=[lbb])
        P.op("dve", lambda e: e.tensor_tensor(lwv[:, 0, :], lbv[:, 0, :], lbv[:, 1, :], ALU.add), reads=[lbb], writes=[lwb])
        P.op("dve", lambda e: e.tensor_tensor(lwv[:, 0, :], lwv[:, 0, :], lbv[:, 2, :], ALU.add), reads=[lbb], writes=[lwb])
        P.op("dve", lambda e: e.tensor_tensor(lwv[:, 0, :], lwv[:, 0, :], lbv[:, 3, :], ALU.add), reads=[lbb], writes=[lwb])
        P.op("dve", lambda e: e.reciprocal(lwv[:, 0, :], lwv[:, 0, :]), reads=[lwb], writes=[lwb])
        P.op("dve", lambda e: e.memset(lwv[:, 1, :], 0.0), writes=[lwb])
        for j in range(1, layer_idx + 1):
            P.op("dve", lambda e, j=j: e.tensor_tensor(lwv[:, 1, :], lwv[:, 1, :], lbv[:, j, :], ALU.add), reads=[lbb], writes=[lwb])
        P.op("dve", lambda e: e.tensor_tensor(lwv[:, 1, :], lwv[:, 1, :], lwv[:, 0, :], ALU.mult), reads=[lwb], writes=[lwb])
        P.op("dve", lambda e: e.tensor_scalar(lwv[:, 2, :], lwv[:, 1, :], -1.0, 1.0, ALU.mult, ALU.add), reads=[lwb], writes=[lwb])

        def f32buf(name, n=NTOK):
            t = P.sb([128, n], F32, name)
            return t, Buf(t)

        def bfbuf(name, n=NTOK):
            t = P.sb([128, n], BF16, name)
            return t, Buf(t)
        qf, qfb = f32buf("qf")
        gate, gateb = f32buf("gate")
        oacc, oaccb = f32buf("oacc")
        A, Ab = f32buf("scrA")
        B, Bb = f32buf("scrB")
        C, Cb = f32buf("scrC")
        Qh, Qhb = bfbuf("Qh")
        Kh, Khb = bfbuf("Kh")
        blt = P.sb([128, NCH], F32, "blt"); bltb = Buf(blt)
        wts = [Buf(P.sb([128, KC * 128], BF16, f"wt{i}")) for i in range(2)]
        Vt = P.sb([128, 18 * 128], BF16, "Vtok"); Vb = Buf(Vt)
        V = Vt[:, :].rearrange("p (b n) -> p b n", b=18)
        sgt, sgb = f32buf("sgtmp", 512)
        dd = []
        for d_ in range(2):
            o = Ctx()
            o.Qt, o.Qtb = bfbuf(f"Qt{d_}")
            o.Kt = P.sb([128, 18 * 128], BF16, f"Ktok{d_}"); o.Ktb = Buf(o.Kt)
            o.Ktv = o.Kt[:, :].rearrange("p (b n) -> p b n", b=18)
            o.As = P.sb([128, 18 * 128], BF16, f"As{d_}"); o.Asb = Buf(o.As)
            o.Asv = o.As[:, :].rearrange("p (b n) -> p b n", b=18)
            o.ebl = P.sb([128, NCH], F32, f"ebl{d_}"); o.eblb = Buf(o.ebl)
            o.S = [Buf(P.sb([128, 128], F32, f"S{d_}{i}")) for i in range(2)]
            o.Sb = [Buf(P.sb([128, 128], BF16, f"Sb{d_}{i}")) for i in range(2)]
            dd.append(o)
        yst = [Buf(P.sb([128, NTOK], BF16, f"yst{i}")) for i in range(2)]
        widx = [0]
        ltiles = [(r, a, 512, r * 1024 + a) for r in range(2) for a in (0, 512)]
        ctiles = [(r, TLAT, 128, 2048 + r * 128) for r in range(2)]
        blocks = [(lb // 8, (lb % 8) * 128) for lb in range(16)] + [(0, TLAT), (1, TLAT)]
        fin = []
        Cv = C[:, :].rearrange("p (n k) -> p n k", k=CH)
        Bv = B[:, :].rearrange("p (n k) -> p n k", k=CH)
        blb = blt[:, :].rearrange("p (n o) -> p n o", o=1).to_broadcast([128, NCH, CH])

        def load_w(sl, hd):
            wb = wts[widx[0] % 2]; widx[0] += 1
            wt = wb.ap.rearrange("p (kc m) -> p kc m", kc=KC)
            P.dma("pool", wt, win[sl].rearrange("(kc p) m -> p kc m", p=128)[:, :, hd * 128:(hd + 1) * 128], writes=[wb])
            return wb, wt

        def proj_fm(sl, hd, evac):
            wb, wt = load_w(sl, hd)
            for (r, a, n, sc0) in ltiles + ctiles:
                pb = next_bank(c)
                fns = [(lambda pe, kc=kc, r=r, a=a, n=n, pb=pb, wt=wt: pe.matmul(
                    pb.ap[:, 0:n], wt[:, kc, :], hv[:, kc, r, a:a + n],
                    start=(kc == 0), stop=(kc == KC - 1))) for kc in range(KC)]
                P.mm(fns, reads=[wb, hb], writes=[pb])
                evac(pb, n, sc0)

        for hd in range(8):
            proj_fm(0, hd, lambda pb, n, sc0: P.op(
                "act", lambda e: e.activation(qf[:, sc0:sc0 + n], pb.ap[:, 0:n], AF.Copy), reads=[pb], writes=[qfb]))

            def evac_g(pb, n, sc0):
                P.op("act", lambda e: e.activation(sgt[:, 0:n], pb.ap[:, 0:n], AF.Sigmoid), reads=[pb], writes=[sgb])
                P.op("dve", lambda e: e.tensor_tensor(gate[:, sc0:sc0 + n], pb.ap[:, 0:n], sgt[:, 0:n], ALU.mult),
                     reads=[pb, sgb], writes=[gateb])
            proj_fm(4, hd, evac_g)
            wb, wt = load_w(3, hd)
            for bi, (r, c0) in enumerate(blocks):
                pb = next_bank(c)
                fns = [(lambda pe, kc=kc, r=r, c0=c0, pb=pb, wt=wt: pe.matmul(
                    pb.ap[:, 0:128], hv[:, kc, r, c0:c0 + 128], wt[:, kc, :],
                    start=(kc == 0), stop=(kc == KC - 1))) for kc in range(KC)]
                P.mm(fns, reads=[wb, hb], writes=[pb])
                P.op("act", lambda e, bi=bi, pb=pb: e.activation(V[:, bi, :], pb.ap[:, 0:128], AF.Copy), reads=[pb], writes=[Vb])
            for d_ in range(2):
                o = dd[d_]
                proj_fm(1 + d_, hd, lambda pb, n, sc0: P.op(
                    "act", lambda e: e.activation(A[:, sc0:sc0 + n], pb.ap[:, 0:n], AF.Sigmoid), reads=[pb], writes=[Ab]))
                lbc = lwv[:, 1, d_ * 8 + hd:d_ * 8 + hd + 1]
                omc = lwv[:, 2, d_ * 8 + hd:d_ * 8 + hd + 1]
                P.op("dve", lambda e, lbc=lbc, omc=omc: e.tensor_scalar(A[:, :], A[:, :], omc, lbc, ALU.mult, ALU.add),
                     reads=[lwb], writes=[Ab])
                P.op("act", lambda e: e.activation(B[:, :], A[:, :], AF.Ln), reads=[Ab], writes=[Bb])
                P.op("dve", lambda e: e.tensor_scalar(A[:, :], A[:, :], -1.0, 1.0, ALU.mult, ALU.add), reads=[], writes=[Ab])
                for ci in range(NCH):
                    P.op("dve", lambda e, ci=ci: e.tensor_tensor_scan(
                        C[:, ci * CH:(ci + 1) * CH], onesf[:, :], B[:, ci * CH:(ci + 1) * CH], 0.0, ALU.mult, ALU.add),
                         reads=[Bb, onesfb], writes=[Cb])
                P.op("dve", lambda e: e.tensor_copy(blt[:, :], Cv[:, :, CH - 1]), reads=[Cb], writes=[bltb])
                P.op("act", lambda e, o=o: e.activation(o.ebl[:, :], blt[:, :], AF.Exp), reads=[bltb], writes=[o.eblb])
                if d_ == 1:
                    P.op("dve", lambda e: e.tensor_tensor(B[:, :], B[:, :], C[:, :], ALU.subtract), reads=[Cb], writes=[Bb])
                    P.op("dve", lambda e: e.tensor_tensor(Bv, Bv, blb, ALU.add), reads=[bltb], writes=[Bb])
                    P.op("dve", lambda e: e.tensor_tensor(Cv, blb, Bv, ALU.subtract), reads=[bltb, Bb], writes=[Cb])
                    Bs, Bsb, Dm, Dmb = B, Bb, C, Cb
                else:
                    P.op("dve", lambda e: e.tensor_tensor(Bv, blb, Cv, ALU.subtract), reads=[bltb, Cb], writes=[Bb])
                    Bs, Bsb, Dm, Dmb = C, Cb, B, Bb
                P.op("act", lambda e, Bs=Bs: e.activation(Bs[:, :], Bs[:, :], AF.Exp), reads=[], writes=[Bsb])
                P.op("dve", lambda e, o=o, Bs=Bs: e.tensor_tensor(o.Qt[:, :], qf[:, :], Bs[:, :], ALU.mult),
                     reads=[qfb, Bsb], writes=[o.Qtb])
                P.op("act", lambda e, Bs=Bs, Dm=Dm: e.activation(Bs[:, :], Dm[:, :], AF.Exp, scale=-1.0),
                     reads=[Dmb], writes=[Bsb])
                P.op("dve", lambda e, Bs=Bs: e.tensor_tensor(Qh[:, :], qf[:, :], Bs[:, :], ALU.mult),
                     reads=[qfb, Bsb], writes=[Qhb])
                P.op("act", lambda e, Dm=Dm: e.activation(Dm[:, :], Dm[:, :], AF.Exp), reads=[], writes=[Dmb])
                P.op("dve", lambda e, Dm=Dm: e.tensor_tensor(Kh[:, :], A[:, :], Dm[:, :], ALU.mult),
                     reads=[Ab, Dmb], writes=[Khb])
                for bi in range(18):
                    c0 = bi * 128
                    pb = next_bank(c)
                    pbt = pb.ap[:, 0:64].bitcast(BF16)
                    P.mm([lambda pe, c0=c0, pbt=pbt: pe.transpose(pbt, Kh[:, c0:c0 + 128], idt[:, :])],
                         reads=[Khb, idb], writes=[pb])
                    P.op("act", lambda e, o=o, bi=bi, pbt=pbt: e.activation(o.Ktv[:, bi, :], pbt, AF.Copy), reads=[pb], writes=[o.Ktb])
                    pb2 = next_bank(c)
                    P.mm([lambda pe, c0=c0, pb2=pb2: pe.matmul(pb2.ap[:, 0:128], Kh[:, c0:c0 + 128], Qh[:, c0:c0 + 128],
                                                              start=True, stop=True)],
                         reads=[Khb, Qhb], writes=[pb2])
                    P.op("dve", lambda e, o=o, bi=bi, pb2=pb2, d_=d_: e.tensor_tensor(
                        o.Asv[:, bi, :], pb2.ap[:, 0:128], mkt[:, d_ * 128:(d_ + 1) * 128], ALU.mult),
                         reads=[pb2, mkb], writes=[o.Asb])
            order = [[16, 17] + list(range(16)), [17, 16] + list(range(15, -1, -1))]
            P.op("dve", lambda e: e.memset(oacc[:, :], 0.0), writes=[oaccb])
            for d_ in range(2):
                o = dd[d_]
                o.cur = 0
                P.op("dve", lambda e, o=o: e.memset(o.S[0].ap[:, :], 0.0), writes=[o.S[0]])
                P.op("dve", lambda e, o=o: e.memset(o.Sb[0].ap[:, :], 0.0), writes=[o.Sb[0]])
            for step in range(18):
                for d_ in range(2):
                    o = dd[d_]
                    bi = order[d_][step]
                    c0 = bi * 128
                    js = (0, 1) if d_ == 0 else (1, 0)
                    pbo = c.banks[(2 * step + d_) % 4]
                    P.mm([lambda pe, o=o, bi=bi, pbo=pbo: pe.matmul(pbo.ap[:, 0:128], V[:, bi, :], o.Asv[:, bi, :], start=True, stop=False)],
                         reads=[Vb, o.Asb], writes=[pbo])
                    for jj, j in enumerate(js):
                        cur = o.cur
                        nxt = 1 - cur
                        ci = bi * 2 + j
                        P.mm([lambda pe, o=o, cur=cur, c0=c0, j=j, jj=jj, pbo=pbo: pe.matmul(
                            pbo.ap[:, j * CH:(j + 1) * CH], o.Sb[cur].ap[:, :], o.Qt[:, c0 + j * CH:c0 + (j + 1) * CH],
                            start=False, stop=(jj == 1))], reads=[o.Sb[cur], o.Qtb], writes=[pbo])
                        pkv = c.banks[4 + (2 * (2 * step + jj) + d_) % 4]
                        P.mm([lambda pe, o=o, bi=bi, j=j, pkv=pkv: pe.matmul(
                            pkv.ap[:, 0:128], o.Ktv[j * CH:(j + 1) * CH, bi, :], V[j * CH:(j + 1) * CH, bi, :],
                            start=True, stop=True)], reads=[o.Ktb, Vb], writes=[pkv])
                        P.op("dve", lambda e, o=o, cur=cur, nxt=nxt, ci=ci, pkv=pkv: e.scalar_tensor_tensor(
                            o.S[nxt].ap[:, :], o.S[cur].ap[:, :], o.ebl[:, ci:ci + 1], pkv.ap[:, 0:128], ALU.mult, ALU.add),
                             reads=[o.S[cur], o.eblb, pkv], writes=[o.S[nxt]])
                        P.op("act", lambda e, o=o, nxt=nxt: e.activation(o.Sb[nxt].ap[:, :], o.S[nxt].ap[:, :], AF.Copy),
                             reads=[o.S[nxt]], writes=[o.Sb[nxt]])
                        o.cur = nxt
                    P.op("dve", lambda e, c0=c0, pbo=pbo: e.tensor_tensor(
                        oacc[:, c0:c0 + 128], pbo.ap[:, 0:128], oacc[:, c0:c0 + 128], ALU.add), reads=[pbo], writes=[oaccb])
            oav = oacc[:, :].rearrange("p (c n) -> p c n", c=1)
            sqr = [Buf(B[:, 0:NTOK // 2].bitcast(BF16)), Buf(B[:, NTOK // 2:NTOK].bitcast(BF16))]
            for s_ in sqr:
                alias_from(s_, Bb); merge_into(s_, Bb)
            rms_rstd(c, oav, oaccb, C, Cb, sqr, ncol=NTOK, dim=128, nch=1)
            for s_ in sqr:
                merge_into(Bb, s_)
            P.op("dve", lambda e: e.tensor_tensor(oacc[:, :], oacc[:, :], C[:, :], ALU.mult), reads=[Cb], writes=[oaccb])
            P.op("dve", lambda e: e.tensor_tensor(oacc[:, :], oacc[:, :], gate[:, :], ALU.mult), reads=[gateb], writes=[oaccb])
            yb = yst[hd % 2]
            P.op("act", lambda e, yb=yb: e.activation(yb.ap[:, :], oacc[:, :], AF.Copy, scale=gnt[:, 0:1]),
                 reads=[oaccb, gnb], writes=[yb])
            for r in range(2):
                dsth = ymo[r, hd * 128:(hd + 1) * 128, :]
                fin.append(P.dma("sp", dsth[:, 0:TLAT], yb.ap[:, r * 1024:(r + 1) * 1024], reads=[yb]))
                fin.append(P.dma("sp", dsth[:, TLAT:T], yb.ap[:, 2048 + r * 128:2048 + (r + 1) * 128], reads=[yb]))
        P.emit(fin)
    return nc


def hgrn_tables():
    import ml_dtypes
    s = np.arange(128)[:, None]
    t = np.arange(128)[None, :]
    same = (s // 64) == (t // 64)
    mf = (same & (s <= t)).astype(np.float32)
    mb = (same & (s >= t)).astype(np.float32)
    return {"ident": np.eye(128).astype(ml_dtypes.bfloat16), "masks": np.ascontiguousarray(np.concatenate([mf, mb], 1))}


def build_mod(nc):
    scTd = nc.dram_tensor("scT", [128, 16 * 5], F32, kind="ExternalInput").ap()
    wmod = nc.dram_tensor("wmod", [4, D, 1536], F32, kind="ExternalInput").ap()
    bmod = nc.dram_tensor("bmod", [128, 48], F32, kind="ExternalInput").ap()
    modo = nc.dram_tensor("modo", [128, 48 * 5], F32, kind="ExternalOutput").ap()
    with contextlib.ExitStack() as st:
        P = Prog(nc, st)
        c = setup_common(P)
        sct = P.sb([128, 80], F32, "sct"); scb = Buf(sct)
        sgt = P.sb([128, 80], F32, "sgt"); sgb = Buf(sgt)
        bmt = P.sb([128, 48], F32, "bmt"); bmb = Buf(bmt)
        mot = P.sb([128, 48 * 5], F32, "mot"); mob = Buf(mot)
        mov = mot[:, :].rearrange("p (w k) -> p w k", k=5)
        scv = sct[:, :].rearrange("p (c k) -> p c k", k=5)
        P.dma("sp", sct[:, :], scTd, writes=[scb])
        P.dma("sp", bmt[:, :], bmod, writes=[bmb])
        P.op("act", lambda e: e.activation(sgt[:, :], sct[:, :], AF.Sigmoid), reads=[scb], writes=[sgb])
        P.op("dve", lambda e: e.tensor_tensor(sct[:, :], sct[:, :], sgt[:, :], ALU.mult), reads=[sgb], writes=[scb])
        wts = [Buf(P.sb([128, KC * 512], F32, f"wt{i}")) for i in range(2)]
        n = 0
        for i in range(4):
            for pc in range(3):
                wb = wts[n % 2]; n += 1
                wt = wb.ap.rearrange("p (kc m) -> p kc m", kc=KC)
                P.dma("sp", wt, wmod[i].rearrange("(kc p) m -> p kc m", p=128)[:, :, pc * 512:(pc + 1) * 512], writes=[wb])
                for mm in range(4):
                    idx = i * 12 + pc * 4 + mm
                    pb = next_bank(c)
                    fns = [(lambda pe, kc=kc, mm=mm, pb=pb, wt=wt: pe.matmul(
                        pb.ap[:, 0:5], wt[:, kc, mm * 128:(mm + 1) * 128], scv[:, kc, :],
                        start=(kc == 0), stop=(kc == KC - 1))) for kc in range(KC)]
                    P.mm(fns, reads=[wb, scb], writes=[pb])
                    P.op("dve", lambda e, idx=idx, pb=pb: e.tensor_scalar(
                        mov[:, idx, :], pb.ap[:, 0:5], bmt[:, idx:idx + 1], None, ALU.add), reads=[pb, bmb], writes=[mob])
        fin = [P.dma("sp", modo, mot[:, :], reads=[mob])]
        P.emit(fin)
    return nc


def _run(nc, maps):
    res = run_bass_kernel_spmd(nc, maps, core_ids=list(range(8)))
    return res.results


def kernel(x, c, ctx, c_ctx, w_mod, b_mod, norm_g, w_mlp_in, w_mlp_out, fnet_w_out, hgrn_w_in, hgrn_lb,
           hgrn_gnorm, hgrn_w_out, diff_w_qkv, diff_lambda, diff_subln, diff_w_out):
    import math
    f32 = np.float32
    x = np.asarray(x, f32); ctx = np.asarray(ctx, f32)
    NL = 4
    cores = [(b, s) for b in range(4) for s in range(2)]
    c5 = np.concatenate([np.asarray(c, f32), np.asarray(c_ctx, f32)[None, :]], 0)
    scT = np.ascontiguousarray(c5.T.reshape(16, 128, 5).transpose(1, 0, 2).reshape(128, 80))
    maps = []
    for j in range(8):
        sl = slice(j * 1536, (j + 1) * 1536)
        maps.append({
            "scT": scT,
            "wmod": np.ascontiguousarray(np.asarray(w_mod, f32)[:, :, sl]),
            "bmod": np.ascontiguousarray(np.asarray(b_mod, f32)[:, sl].reshape(4, 12, 128).transpose(2, 0, 1).reshape(128, 48)),
        })
    nc = bass.Bass("TRN2", target_bir_lowering=False)
    res = _run(build_mod(nc), maps)
    mod_full = np.empty((4, 12288, 5), f32)
    for j in range(8):
        mo = res[j]["modo"].reshape(128, 4, 12, 5)
        mod_full[:, j * 1536:(j + 1) * 1536, :] = mo.transpose(1, 2, 0, 3).reshape(4, 1536, 5)

    def modc(li_list, b):
        outs = []
        for li in li_list:
            m = mod_full[li][:, [b, 4]].reshape(96, 128, 2).transpose(1, 0, 2)
            outs.append(m)
        return np.ascontiguousarray(np.stack(outs, 1).reshape(128, len(li_list) * 96 * 2))

    def gnl(li_list):
        ng = np.asarray(norm_g, f32)
        outs = [ng[li].reshape(4, 16, 128).transpose(2, 0, 1) for li in li_list]
        return np.ascontiguousarray(np.stack(outs, 1).reshape(128, len(li_list) * 64))

    xT = []
    for (b, s) in cores:
        tok = np.concatenate([x[b, s * 1024:(s + 1) * 1024], ctx[b, s * 128:(s + 1) * 128]], 0)
        xT.append(np.ascontiguousarray(tok.T))
    nc = bass.Bass("TRN2", target_bir_lowering=False)
    build_rowstage(nc, False, True, None)
    res = _run(nc, [{"xT": xT[ci], "modc": modc([0], b), "gn": gnl([0])} for ci, (b, s) in enumerate(cores)])
    hT = [res[ci]["ho"] for ci in range(8)]
    ftab = fnet_tables()
    rtab = rope_tables()
    htab = hgrn_tables()
    for i in range(NL):
        mixer, slot = i % 3, i // 3
        maps = []
        for ci, (b, s) in enumerate(cores):
            hall = np.ascontiguousarray(np.stack([hT[2 * b], hT[2 * b + 1]]))
            if mixer == 0:
                m = dict(ftab)
                m["hmy"] = np.ascontiguousarray(hall[:, s * 1024:(s + 1) * 1024, :])
            elif mixer == 1:
                m = dict(htab)
                m["hall"] = hall
                wi = np.asarray(hgrn_w_in, f32)[slot]
                m["win"] = np.ascontiguousarray(np.stack([wi[:, j * 2048 + s * 1024: j * 2048 + (s + 1) * 1024] for j in range(5)]))
                lb = np.asarray(hgrn_lb, f32)[:, :, s * 1024:(s + 1) * 1024]
                m["lbp"] = np.ascontiguousarray(lb.reshape(4, 2, 8, 128).transpose(3, 0, 1, 2).reshape(128, 64))
                m["gnorm"] = np.ascontiguousarray(np.asarray(hgrn_gnorm, f32)[slot].reshape(128, 1))
            else:
                m = dict(rtab)
                m["hall"] = hall
                wq = np.asarray(diff_w_qkv, f32)[slot]
                m["wqkv"] = np.ascontiguousarray(np.stack([wq[:, j * 2048 + s * 1024: j * 2048 + (s + 1) * 1024] for j in range(3)]))
                m["lamT"] = np.ascontiguousarray(np.asarray(diff_lambda, f32)[slot].T)
                m["subln"] = np.ascontiguousarray(np.asarray(diff_subln, f32)[slot].reshape(2, 128).T)
            maps.append(m)
        nc = bass.Bass("TRN2", target_bir_lowering=False)
        if mixer == 0:
            build_fnet(nc)
        elif mixer == 1:
            build_hgrn(nc, i)
        else:
            build_attn(nc, 0.8 - 0.6 * math.exp(-0.3 * i))
        res = _run(nc, maps)
        ymo = [res[ci]["ymo"] for ci in range(8)]
        wout = (fnet_w_out, hgrn_w_out, diff_w_out)[mixer]
        wout = np.ascontiguousarray(np.asarray(wout, f32)[slot])
        w1 = np.ascontiguousarray(np.asarray(w_mlp_in, f32)[i])
        w2 = np.ascontiguousarray(np.asarray(w_mlp_out, f32)[i])
        last = (i == NL - 1)
        lis = [i] if last else [i, i + 1]
        maps = []
        for ci, (b, s) in enumerate(cores):
            ym = np.ascontiguousarray(np.concatenate([ymo[2 * b][s], ymo[2 * b + 1][s]], 0))
            maps.append({"xT": xT[ci], "ym": ym, "modc": modc(lis, b), "gn": gnl(lis), "wout": wout, "w1": w1, "w2": w2})
        nc = bass.Bass("TRN2", target_bir_lowering=False)
        build_rowstage(nc, True, not last, None)
        res = _run(nc, maps)
        xT = [res[ci]["xo"] for ci in range(8)]
        if not last:
            hT = [res[ci]["ho"] for ci in range(8)]
    out = np.empty((4, 2048, 2048), f32)
    for ci, (b, s) in enumerate(cores):
        out[b, s * 1024:(s + 1) * 1024, :] = xT[ci][:, :1024].T
    return out
```
